# Optimizing a Trainium2 kernel written in Bass

```python
import math
import jax, jax.numpy as jnp
from jax import lax
import numpy as np


D_MODEL = 2048
BATCH = 1
SEQ = 16384
DEPTH = 4

HEAD_DIM = 128
ROPE_THETA = 10000.0
RMS_EPS = 1e-6
Q_BLOCK = 128
GRID_W = 64
MLA_HEADS = 4
MLA_Q_LORA = 512
MLA_KV_LORA = 512
MLA_NOPE = 128
MLA_ROPE = 64
MLA_V = 128
DIFF_HEADS = 4
DIFF_QK = HEAD_DIM // 2
SWA_HEADS = 4
SWA_KV_HEADS = 2
SWA_WINDOW = 128
SWA_BLOCK = 128
AX_HEADS = 4
AX_KV_HEADS = 2
N_BRANCH = 4
BRANCH_W = 4 * HEAD_DIM
FFN_DIM = ((8 * D_MODEL + 3 * 256 - 1) // (3 * 256)) * 256
PLE_DIM = 256

MLA_COLS = MLA_Q_LORA + MLA_KV_LORA + MLA_ROPE
DIFF_COLS = 3 * DIFF_HEADS * HEAD_DIM
SWA_COLS = (SWA_HEADS + 2 * SWA_KV_HEADS) * HEAD_DIM
AX_COLS = (AX_HEADS + 2 * AX_KV_HEADS) * HEAD_DIM
IN_COLS = MLA_COLS + DIFF_COLS + SWA_COLS + AX_COLS
IN_SPLITS = (MLA_COLS, MLA_COLS + DIFF_COLS, MLA_COLS + DIFF_COLS + SWA_COLS)

kernel_name = 'hybrid_parallel_gated_encoder'


def _rmsnorm(x, g):
    xf = x.astype(jnp.float32)
    y = xf * lax.rsqrt(jnp.mean(xf * xf, axis=-1, keepdims=True) + RMS_EPS)
    return (y * g.astype(jnp.float32)).astype(x.dtype)


def _rope_tables(pos, dim):
    inv_freq = ROPE_THETA ** (-jnp.arange(0, dim, 2, dtype=jnp.float32) / dim)
    ang = pos.astype(jnp.float32)[:, None] * inv_freq[None, :]
    return jnp.cos(ang), jnp.sin(ang)


def _apply_rope(x, cs):
    cos, sin = cs
    c = cos[None, :, None, :].astype(x.dtype)
    s = sin[None, :, None, :].astype(x.dtype)
    x1, x2 = jnp.split(x, 2, axis=-1)
    return jnp.concatenate([x1 * c - x2 * s, x2 * c + x1 * s], axis=-1)


def _axial_rope(x, row_cs, col_cs):
    xr, xc = jnp.split(x, 2, axis=-1)
    return jnp.concatenate([_apply_rope(xr, row_cs), _apply_rope(xc, col_cs)], axis=-1)


def _to_blocks(t):
    b, s = t.shape[:2]
    return jnp.swapaxes(t.reshape(b, s // Q_BLOCK, Q_BLOCK, *t.shape[2:]), 0, 1)


def _from_blocks(t):
    nb, b, qb = t.shape[:3]
    return jnp.swapaxes(t, 0, 1).reshape(b, nb * qb, *t.shape[3:])


def _blocked_attention(q, k, v, scale):
    b, s, hq, dk = q.shape
    hkv = k.shape[2]
    qg = q.reshape(b, s, hkv, hq // hkv, dk)

    def step(qb):
        sc = jnp.einsum('bqhgd,bkhd->bhgqk', qb, k).astype(jnp.float32) * scale
        pr = jax.nn.softmax(sc, axis=-1).astype(v.dtype)
        return jnp.einsum('bhgqk,bkhe->bqhge', pr, v)

    out = _from_blocks(lax.map(step, _to_blocks(qg)))
    return out.reshape(b, s, hq * v.shape[-1])


def _banded_attention(q, k, v, sink, scale):
    b, s, hq, d = q.shape
    hkv = k.shape[2]
    g = hq // hkv
    wb = SWA_BLOCK
    nb = s // wb
    qb = q.reshape(b, nb, wb, hkv, g, d)

    def band(t):
        tb = t.reshape(b, nb, wb, hkv, t.shape[-1])
        tp = jnp.pad(tb, ((0, 0), (1, 1), (0, 0), (0, 0), (0, 0)))
        return jnp.concatenate([tp[:, :-2], tp[:, 1:-1], tp[:, 2:]], axis=2)

    kb, vb = band(k), band(v)
    sc = jnp.einsum('bnqhgd,bnkhd->bnhgqk', qb, kb).astype(jnp.float32) * scale
    blk = jnp.arange(nb)
    qpos = blk[:, None] * wb + jnp.arange(wb)[None, :]
    kpos = blk[:, None] * wb - wb + jnp.arange(3 * wb)[None, :]
    rel = kpos[:, None, :] - qpos[:, :, None]
    valid = (jnp.abs(rel) <= SWA_WINDOW) & (kpos[:, None, :] >= 0) & (kpos[:, None, :] < s)
    sc = jnp.where(valid[None, :, None, None], sc, -jnp.inf)
    sink_col = jnp.broadcast_to(sink.astype(jnp.float32).reshape(1, 1, hkv, g, 1, 1), (b, nb, hkv, g, wb, 1))
    pr = jax.nn.softmax(jnp.concatenate([sc, sink_col], axis=-1), axis=-1)[..., :-1]
    out = jnp.einsum('bnhgqk,bnkhd->bnqhgd', pr.astype(v.dtype), vb)
    return out.reshape(b, s, hq * d)


def _mla(z, qa_norm, w_uq, kva_norm, w_ukv, rope_cs):
    b, s, _ = z.shape
    c_q, c_kv, k_r = jnp.split(z, [MLA_Q_LORA, MLA_Q_LORA + MLA_KV_LORA], axis=-1)
    q = (_rmsnorm(c_q, qa_norm) @ w_uq).reshape(b, s, MLA_HEADS, MLA_NOPE + MLA_ROPE)
    q = jnp.concatenate([q[..., :MLA_NOPE], _apply_rope(q[..., MLA_NOPE:], rope_cs)], axis=-1)
    kv = (_rmsnorm(c_kv, kva_norm) @ w_ukv).reshape(b, s, MLA_HEADS, MLA_NOPE + MLA_V)
    k_r = _apply_rope(k_r.reshape(b, s, 1, MLA_ROPE), rope_cs)
    k = jnp.concatenate([kv[..., :MLA_NOPE], jnp.broadcast_to(k_r, (b, s, MLA_HEADS, MLA_ROPE))], axis=-1)
    v = kv[..., MLA_NOPE:]
    return _blocked_attention(q, k, v, (MLA_NOPE + MLA_ROPE) ** -0.5)


def _diff(z, lam_params, subln, lam_init, rope_cs):
    b, s, _ = z.shape
    hw = DIFF_HEADS * HEAD_DIM
    q, k, v = jnp.split(z, [hw, 2 * hw], axis=-1)
    q = _apply_rope(q.reshape(b, s, DIFF_HEADS * 2, DIFF_QK), rope_cs).reshape(b, s, DIFF_HEADS, 2, DIFF_QK)
    k = _apply_rope(k.reshape(b, s, DIFF_HEADS * 2, DIFF_QK), rope_cs).reshape(b, s, DIFF_HEADS, 2, DIFF_QK)
    v = v.reshape(b, s, DIFF_HEADS, HEAD_DIM)
    q1, q2 = q[:, :, :, 0], q[:, :, :, 1]
    k1, k2 = k[:, :, :, 0], k[:, :, :, 1]
    lp = lam_params.astype(jnp.float32)
    lam = jnp.exp(jnp.sum(lp[0] * lp[1])) - jnp.exp(jnp.sum(lp[2] * lp[3])) + lam_init
    scale = DIFF_QK ** -0.5

    def step(blk):
        q1b, q2b = blk
        a1 = jax.nn.softmax(jnp.einsum('bqhd,bkhd->bhqk', q1b, k1).astype(jnp.float32) * scale, axis=-1)
        a2 = jax.nn.softmax(jnp.einsum('bqhd,bkhd->bhqk', q2b, k2).astype(jnp.float32) * scale, axis=-1)
        return jnp.einsum('bhqk,bkhe->bqhe', (a1 - lam * a2).astype(v.dtype), v)

    o = _from_blocks(lax.map(step, (_to_blocks(q1), _to_blocks(q2))))
    o = _rmsnorm(o, subln) * (1.0 - lam_init)
    return o.reshape(b, s, DIFF_HEADS * HEAD_DIM)


def _swa(z, sink, rope_cs):
    b, s, _ = z.shape
    q, k, v = jnp.split(z, [SWA_HEADS * HEAD_DIM, (SWA_HEADS + SWA_KV_HEADS) * HEAD_DIM], axis=-1)
    q = _apply_rope(q.reshape(b, s, SWA_HEADS, HEAD_DIM), rope_cs)
    k = _apply_rope(k.reshape(b, s, SWA_KV_HEADS, HEAD_DIM), rope_cs)
    v = v.reshape(b, s, SWA_KV_HEADS, HEAD_DIM)
    return _banded_attention(q, k, v, sink, HEAD_DIM ** -0.5)


def _axial(z, q_norm, k_norm, row_cs, col_cs):
    b, s, _ = z.shape
    q, k, v = jnp.split(z, [AX_HEADS * HEAD_DIM, (AX_HEADS + AX_KV_HEADS) * HEAD_DIM], axis=-1)
    q = _axial_rope(_rmsnorm(q.reshape(b, s, AX_HEADS, HEAD_DIM), q_norm), row_cs, col_cs)
    k = _axial_rope(_rmsnorm(k.reshape(b, s, AX_KV_HEADS, HEAD_DIM), k_norm), row_cs, col_cs)
    v = v.reshape(b, s, AX_KV_HEADS, HEAD_DIM)
    return _blocked_attention(q, k, v, HEAD_DIM ** -0.5)


def setup_inputs(seed: int = 0) -> dict:
    key = jax.random.key(seed)
    ks = jax.random.split(key, 24)
    f32 = jnp.float32

    def dense(k, shape, fan_in):
        return jax.random.normal(k, shape, f32) * (fan_in ** -0.5)

    def gain(k, shape):
        return 1.0 + 0.05 * jax.random.normal(k, shape, f32)

    return {
        'x': jax.random.normal(ks[0], (BATCH, SEQ, D_MODEL), f32),
        'p': jax.random.normal(ks[1], (DEPTH, BATCH, SEQ, PLE_DIM), f32),
        'norm_mix_pre': gain(ks[2], (DEPTH, D_MODEL)),
        'norm_mix_post': gain(ks[3], (DEPTH, D_MODEL)),
        'norm_ffn_pre': gain(ks[4], (DEPTH, D_MODEL)),
        'norm_ffn_post': gain(ks[5], (DEPTH, D_MODEL)),
        'norm_ple_post': gain(ks[6], (DEPTH, D_MODEL)),
        'w_in': dense(ks[7], (DEPTH, D_MODEL, IN_COLS), D_MODEL),
        'mla_qa_norm': gain(ks[8], (DEPTH, MLA_Q_LORA)),
        'mla_w_uq': dense(ks[9], (DEPTH, MLA_Q_LORA, MLA_HEADS * (MLA_NOPE + MLA_ROPE)), MLA_Q_LORA),
        'mla_kva_norm': gain(ks[10], (DEPTH, MLA_KV_LORA)),
        'mla_w_ukv': dense(ks[11], (DEPTH, MLA_KV_LORA, MLA_HEADS * (MLA_NOPE + MLA_V)), MLA_KV_LORA),
        'diff_lambda': 0.1 * jax.random.normal(ks[12], (DEPTH, 4, DIFF_QK), f32),
        'diff_subln': gain(ks[13], (DEPTH, HEAD_DIM)),
        'swa_sink': 0.5 * jax.random.normal(ks[14], (DEPTH, SWA_HEADS), f32),
        'ax_q_norm': gain(ks[15], (DEPTH, HEAD_DIM)),
        'ax_k_norm': gain(ks[16], (DEPTH, HEAD_DIM)),
        'w_branch': dense(ks[17], (DEPTH, N_BRANCH, BRANCH_W, D_MODEL), BRANCH_W),
        'w_branch_gate': dense(ks[18], (DEPTH, N_BRANCH, D_MODEL, D_MODEL), D_MODEL),
        'w_o': dense(ks[19], (DEPTH, D_MODEL, D_MODEL), D_MODEL),
        'w_ffn_in': dense(ks[20], (DEPTH, D_MODEL, 2 * FFN_DIM), D_MODEL),
        'w_ffn_out': dense(ks[21], (DEPTH, FFN_DIM, D_MODEL), FFN_DIM),
        'w_ple': dense(ks[22], (DEPTH, PLE_DIM, D_MODEL), PLE_DIM),
        'w_ple_gate': dense(ks[23], (DEPTH, D_MODEL, D_MODEL), D_MODEL),
    }


def reference(x, p, norm_mix_pre, norm_mix_post, norm_ffn_pre, norm_ffn_post, norm_ple_post,
              w_in, mla_qa_norm, mla_w_uq, mla_kva_norm, mla_w_ukv, diff_lambda, diff_subln,
              swa_sink, ax_q_norm, ax_k_norm, w_branch, w_branch_gate, w_o,
              w_ffn_in, w_ffn_out, w_ple, w_ple_gate):
    b, s, _ = x.shape
    rows = s // GRID_W
    t = jnp.arange(s)
    row_pos = jnp.broadcast_to(jnp.arange(rows)[:, None], (rows, GRID_W)).reshape(s)
    col_pos = jnp.broadcast_to(jnp.arange(GRID_W)[None, :], (rows, GRID_W)).reshape(s)
    rope_mla = _rope_tables(t, MLA_ROPE)
    rope_diff = _rope_tables(t, DIFF_QK)
    rope_swa = _rope_tables(t, HEAD_DIM)
    rope_row = _rope_tables(row_pos, HEAD_DIM // 2)
    rope_col = _rope_tables(col_pos, HEAD_DIM // 2)

    for i in range(DEPTH):
        lam_init = 0.8 - 0.6 * math.exp(-0.3 * i)
        h = _rmsnorm(x, norm_mix_pre[i])
        z_a, z_b, z_c, z_d = jnp.split(h @ w_in[i], IN_SPLITS, axis=-1)
        branches = (
            _mla(z_a, mla_qa_norm[i], mla_w_uq[i], mla_kva_norm[i], mla_w_ukv[i], rope_mla),
            _diff(z_b, diff_lambda[i], diff_subln[i], lam_init, rope_diff),
            _swa(z_c, swa_sink[i], rope_swa),
            _axial(z_d, ax_q_norm[i], ax_k_norm[i], rope_row, rope_col),
        )
        merged = jax.nn.sigmoid(h @ w_branch_gate[i, 0]) * (branches[0] @ w_branch[i, 0])
        for j in range(1, N_BRANCH):
            merged = merged + jax.nn.sigmoid(h @ w_branch_gate[i, j]) * (branches[j] @ w_branch[i, j])
        x = x + _rmsnorm(merged @ w_o[i], norm_mix_post[i])
        h = _rmsnorm(x, norm_ffn_pre[i])
        gate, up = jnp.split(h @ w_ffn_in[i], 2, axis=-1)
        x = x + _rmsnorm((jax.nn.silu(gate) * up) @ w_ffn_out[i], norm_ffn_post[i])
        ple = jax.nn.sigmoid(x @ w_ple_gate[i]) * (p[i] @ w_ple[i])
        x = x + _rmsnorm(ple, norm_ple_post[i])
    return x
```

```python
import math
from contextlib import ExitStack
import numpy as np
import ml_dtypes
import concourse.bass as bass
import concourse.mybir as mybir
from concourse.bass_utils import run_bass_kernel_spmd

F32 = mybir.dt.float32
BF16 = mybir.dt.bfloat16
ALU = mybir.AluOpType
AF = mybir.ActivationFunctionType

NCORES = 8
D = 2048
KC = 16
DEPTH = 4
TT = 512
FFN = 5632
HC = 44
EPS = 1e-6
NGL = 93
import os
KSTOP = int(os.environ.get("KSTOP", "99"))


class Buf:
    __slots__ = ("name", "last_write", "readers")

    def __init__(self, name, lw=None):
        self.name = name
        self.last_write = lw
        self.readers = []


class Op:
    __slots__ = ("eng", "fn", "is_dma", "deps", "signal", "cnt", "key", "inc", "seq")

    def __init__(self, eng, fn, is_dma):
        self.eng = eng
        self.fn = fn
        self.is_dma = is_dma
        self.deps = []
        self.signal = is_dma
        self.cnt = 0
        self.key = ("d", eng) if is_dma else ("c", eng)
        self.inc = 16 if is_dma else 1


ENGS = ("pe", "act", "dve", "pool", "sp")
NSLOT = 8


class Prog:
    def __init__(self):
        self.ops = {e: [] for e in ENGS}
        self.bufs = []
        self.pbufs = []
        self.last_barrier = None
        self.ndma = {e: 0 for e in ENGS}
        self.last_dma = {e: {} for e in ENGS}

    def buf(self, name, persistent=False):
        b = Buf(name, self.last_barrier)
        (self.pbufs if persistent else self.bufs).append(b)
        return b

    def op(self, eng, fn, reads=(), writes=(), dma=False):
        o = Op(eng, fn, dma)
        deps = set()
        for b in reads:
            if b.last_write is not None:
                deps.add(b.last_write)
        for b in writes:
            if b.last_write is not None:
                deps.add(b.last_write)
            for r in b.readers:
                deps.add(r)
        o.seq = len(self.ops[eng])
        if dma:
            k = self.ndma[eng]
            self.ndma[eng] = k + 1
            o.key = ("d", eng, k % NSLOT)
            o.cnt = 16 * (k // NSLOT + 1)
            prev = self.last_dma[eng].get(k % NSLOT)
            if prev is not None:
                deps.add(prev)
            self.last_dma[eng][k % NSLOT] = o
        best = {}
        for d in deps:
            if d.eng == eng and eng == "pe" and not d.is_dma and not dma:
                continue
            if d.key not in best or d.seq > best[d.key].seq:
                best[d.key] = d
        for d in best.values():
            d.signal = True
            o.deps.append(d)
        for b in reads:
            b.readers.append(o)
        for b in writes:
            b.last_write = o
            b.readers = []
        self.ops[eng].append(o)
        return o

    def barrier(self, fn):
        o = self.op("dve", fn, reads=(), writes=list(self.bufs) + list(self.pbufs))
        for e in ENGS:
            los = [x for x in self.ops[e][-1:] if not x.is_dma] + list(self.last_dma[e].values())
            for lo in los:
                if lo is not o and lo not in o.deps:
                    lo.signal = True
                    o.deps.append(lo)
        o.signal = True
        self.last_barrier = o
        self.bufs = []
        return o

    def emit(self, nc, stack):
        sems = {}
        for e in ("pe", "act", "dve", "pool"):
            sems[("c", e)] = stack.enter_context(nc.semaphore("c_" + e))
        for e in ("sp", "pool"):
            for i in range(NSLOT):
                sems[("d", e, i)] = stack.enter_context(nc.semaphore(f"d_{e}{i}"))
        tot = {}
        for e in ENGS:
            for o in self.ops[e]:
                if o.is_dma:
                    tot[o.key] = max(tot.get(o.key, 0), o.cnt)
                elif o.signal:
                    tot[o.key] = tot.get(o.key, 0) + o.inc
                    o.cnt = tot[o.key]
        block = stack.enter_context(nc.Block())
        ops = self.ops

        def run(eng_name, eng):
            known = {}
            for o in ops[eng_name]:
                need = {}
                for d in o.deps:
                    if d.cnt > need.get(d.key, 0):
                        need[d.key] = d.cnt
                for key, cnt in need.items():
                    if known.get(key, 0) >= cnt:
                        continue
                    known[key] = cnt
                    eng.wait_ge(sems[key], cnt)
                ins = o.fn(eng)
                if o.signal:
                    ins.then_inc(sems[o.key], o.inc)
            for i in range(NSLOT):
                key = ("d", eng_name, i)
                if tot.get(key, 0):
                    eng.wait_ge(sems[key], tot[key])

        @block.sync
        def _(e):
            run("sp", e)

        @block.tensor
        def _(e):
            run("pe", e)

        @block.scalar
        def _(e):
            run("act", e)

        @block.vector
        def _(e):
            run("dve", e)

        @block.gpsimd
        def _(e):
            run("pool", e)


MLA_COLS = 1088
DIFF0 = 1088
SWA0 = 2624
AX0 = 3648


def _sw(cols, block):
    cols = np.asarray(cols)
    c = cols.reshape(-1, 2, block // 2)
    return c[:, ::-1, :].reshape(-1)


def _inx_cols():
    ch = []
    r = np.arange
    for c in range(4):
        ch.append(r(c * 128, (c + 1) * 128))
    for c in range(4):
        ch.append(r(512 + c * 128, 512 + (c + 1) * 128))
    kr = r(1024, 1088)
    ch.append(np.concatenate([kr, kr]))
    ch.append(np.concatenate([_sw(kr, 64), _sw(kr, 64)]))

    def pairs(base, n, blk):
        for c in range(n):
            x = r(base + c * 128, base + (c + 1) * 128)
            ch.append(x)
            ch.append(_sw(x, blk))
    pairs(DIFF0, 4, 64)
    pairs(DIFF0 + 512, 4, 64)
    pairs(SWA0, 4, 128)
    pairs(SWA0 + 512, 2, 128)
    pairs(AX0, 4, 64)
    pairs(AX0 + 512, 2, 64)
    cols = np.concatenate(ch)
    v0 = r(DIFF0 + 1024, DIFF0 + 1536)
    v1 = np.concatenate([r(SWA0 + 768, SWA0 + 1024), r(AX0 + 768, AX0 + 1024)])
    return np.concatenate([cols, v0, v1])


N_INX_FM = 50
NCOLX = N_INX_FM * 128 + 1024


def _kgroup(w, gc):
    K, N = w.shape
    return np.ascontiguousarray(w.reshape(K // 128, 128, N // gc, gc).transpose(2, 1, 0, 3))


def _host_weights(inp, layers):
    out = {}
    cols = _inx_cols()
    r = np.arange
    uq_cols = []
    for h in range(4):
        uq_cols.append(r(h * 192, h * 192 + 128))
    p01 = np.concatenate([r(128, 192), r(192 + 128, 192 + 192)])
    p23 = np.concatenate([r(2 * 192 + 128, 3 * 192), r(3 * 192 + 128, 4 * 192)])
    uq_cols += [p01, _sw(p01, 64), p23, _sw(p23, 64)]
    uq_cols = np.concatenate(uq_cols)
    ukv_cols = np.concatenate([r(h * 256, h * 256 + 128) for h in range(4)] +
                              [r(h * 256 + 128, h * 256 + 256) for h in range(4)])
    for L in layers:
        o = {}
        wi = inp["w_in"][L][:, cols]
        o["winx"] = np.ascontiguousarray(
            np.concatenate([_kgroup(wi[:, :6144], 512).reshape(12, 128, -1),
                            np.pad(_kgroup(wi[:, 6144:6400], 256), ((0, 0), (0, 0), (0, 0), (0, 256))).reshape(1, 128, -1),
                            _kgroup(wi[:, 6400:], 512).reshape(2, 128, -1)], 0))
        o["wuq"] = _kgroup(inp["mla_w_uq"][L][:, uq_cols], 512).reshape(2, 128, -1)
        o["wukv"] = _kgroup(inp["mla_w_ukv"][L][:, ukv_cols], 512).reshape(2, 128, -1)
        wg = inp["w_branch_gate"][L]
        o["wg"] = np.ascontiguousarray(
            wg.reshape(4, 16, 128, 16, 128).transpose(3, 2, 0, 1, 4)).reshape(16, 128, -1)
        wb = inp["w_branch"][L]
        o["wb"] = np.ascontiguousarray(
            wb.reshape(4, 4, 128, 16, 128).transpose(3, 2, 0, 1, 4)).reshape(16, 128, -1)
        o["wo"] = _kgroup(inp["w_o"][L], 512).reshape(4, 128, -1)
        wf = inp["w_ffn_in"][L]
        g = _kgroup(wf[:, :FFN], 256)
        u = _kgroup(wf[:, FFN:], 256)
        o["wfi"] = np.ascontiguousarray(np.stack([g, u], 2)).reshape(22, 128, -1)
        o["wfo"] = _kgroup(inp["w_ffn_out"][L], 128).reshape(16, 128, -1)
        o["wpg"] = _kgroup(inp["w_ple_gate"][L], 512).reshape(4, 128, -1)
        o["wpl"] = _kgroup(inp["w_ple"][L], 2048).reshape(1, 128, -1)
        out[L] = o
    return out


def _gains(inp):
    cols = []
    for L in range(DEPTH):
        for nm in ("norm_mix_pre", "norm_mix_post", "norm_ffn_pre", "norm_ffn_post", "norm_ple_post"):
            cols.append(inp[nm][L].reshape(16, 128).T)
        cols.append(inp["mla_qa_norm"][L].reshape(4, 128).T)
        cols.append(inp["mla_kva_norm"][L].reshape(4, 128).T)
        cols.append(inp["diff_subln"][L].reshape(1, 128).T)
        for nm in ("ax_q_norm", "ax_k_norm"):
            g = inp[nm][L]
            cols.append(g.reshape(1, 128).T)
            cols.append(g[_sw(np.arange(128), 64)].reshape(1, 128).T)
    return np.ascontiguousarray(np.concatenate(cols, 1).astype(np.float32))


G_MIXPRE, G_MIXPOST, G_FFNPRE, G_FFNPOST, G_PLEPOST, G_QA, G_KVA, G_SUBLN, G_AXQ, G_AXQS, G_AXK, G_AXKS = (
    0, 16, 32, 48, 64, 80, 84, 88, 89, 90, 91, 92)


def _rope_tabs(S):
    t = np.arange(S, dtype=np.float32)

    def tab(pos, dim):
        inv = (10000.0 ** (-np.arange(0, dim, 2, dtype=np.float32) / dim)).astype(np.float32)
        ang = pos.astype(np.float32)[:, None] * inv[None, :]
        c, s = np.cos(ang).astype(np.float32), np.sin(ang).astype(np.float32)
        cf = np.concatenate([c, c], 1).T
        sf = np.concatenate([-s, s], 1).T
        return cf, sf
    c64, s64 = tab(t, 64)
    c128, s128 = tab(t, 128)
    row = np.floor(t / 64)
    col = t - row * 64
    cr, sr = tab(row, 64)
    cc, sc = tab(col, 64)
    return np.stack([np.concatenate([c64, c64]), np.concatenate([s64, s64]), c128, s128,
                     np.concatenate([cr, cc]), np.concatenate([sr, sc])]).astype(np.float32)


def _swa_masks(core, ncores, ntiles):
    i = np.arange(128)[:, None]
    j = np.arange(512)[None, :]
    ms = []
    for m in range(-1, 5):
        ms.append((np.abs(m * 128 + i - j) <= 128).astype(np.float32))
    first = ms[0] * (0.0 if core == 0 else 1.0)
    last = ms[5] * (0.0 if core == ncores - 1 else 1.0)
    return np.stack(ms + [first, last]).astype(ml_dtypes.bfloat16)


NQ = 18
NKSEC = 11
NVSEC = 2


class WSrc:
    def __init__(self, ap, buf):
        self.ap_, self.buf = ap, buf

    def rearrange(self, *a, **k):
        return WSrc(self.ap_.rearrange(*a, **k), self.buf)


class WT:
    def __init__(self, twin, bufs):
        self.twin, self.bufs = twin, bufs

    def ap(self):
        return self

    def __getitem__(self, g):
        return WSrc(self.twin.ap()[g], self.bufs[g])


class Builder:
    def __init__(self, tok, phases, final):
        self.tok = tok
        self.nt = tok // TT
        self.stot = tok * NCORES
        self.phases = phases
        self.nc = bass.Bass("TRN2", target_bir_lowering=False)
        self.P = Prog()
        self.st = ExitStack()
        self.uid = 0
        self.dram = {}
        self.dbuf = {}

    def ext_in(self, name, shape, dt=F32):
        t = self.nc.dram_tensor(name, list(shape), dt, kind="ExternalInput")
        self.dram[name] = t
        return t

    def ext_out(self, name, shape, dt=F32):
        t = self.nc.dram_tensor(name, list(shape), dt, kind="ExternalOutput")
        self.dram[name] = t
        return t

    def db(self, key):
        if key not in self.dbuf:
            self.dbuf[key] = self.P.buf(str(key), persistent=True)
        return self.dbuf[key]

    def setup_sbuf(self):
        nc = self.nc
        AR = 103 * 1024
        self.arena = self.st.enter_context(nc.sbuf_tensor("arena", [128, AR], BF16))
        self.off = 0
        self.ps = [self.st.enter_context(nc.psum_tensor(f"ps{i}", [128, 512], F32)) for i in range(8)]

    def carve(self, nbytes):
        o = self.off
        self.off += nbytes
        assert self.off <= 103 * 1024 * 2, self.off
        return o

    def vb(self, off, n):
        return self.arena[:, off // 2: off // 2 + n]

    def vf(self, off, n):
        return self.arena[:, off // 2: off // 2 + 2 * n].bitcast(F32)


def build_program(tok, phases, final, first):
    B = Builder(tok, phases, final)
    nc, P = B.nc, B.P
    nt = B.nt
    NKB = tok // 128
    layersA = [L for (k, L) in phases if k == "A"]
    layersBC = [L for (k, L) in phases if k == "BC"]
    allL = sorted(set(layersA + layersBC))

    xin = B.ext_in("xin", [D, tok])
    gv_d = B.ext_in("gv", [128, DEPTH * NGL])
    if layersBC:
        lam_d = B.ext_in("lamv", [128, DEPTH * 256])
        sink_d = B.ext_in("sinkv", [128, DEPTH * 4])
    W = {}
    for L in allL:
        W[L] = {}
        if L in layersA:
            W[L]["winx"] = B.ext_in(f"winx{L}", [15, 128, 16 * 512])
            W[L]["wuq"] = B.ext_in(f"wuq{L}", [2, 128, 4 * 512])
            W[L]["wukv"] = B.ext_in(f"wukv{L}", [2, 128, 4 * 512])
        if L in layersBC:
            W[L]["wg"] = B.ext_in(f"wg{L}", [16, 128, 4 * 16 * 128])
            W[L]["wb"] = B.ext_in(f"wb{L}", [16, 128, 4 * 4 * 128])
            W[L]["wo"] = B.ext_in(f"wo{L}", [4, 128, 16 * 512])
            W[L]["wfi"] = B.ext_in(f"wfi{L}", [22, 128, 2 * 16 * 256])
            W[L]["wfo"] = B.ext_in(f"wfo{L}", [16, 128, 44 * 128])
            W[L]["wpg"] = B.ext_in(f"wpg{L}", [4, 128, 16 * 512])
            W[L]["wpl"] = B.ext_in(f"wpl{L}", [1, 128, 2 * 2048])
            W[L]["pfm"] = B.ext_in(f"pfm{L}", [256, tok])
    if layersA:
        tabs_d = B.ext_in("tabs", [6, 128, tok])
    if layersBC:
        swm_d = B.ext_in("swm", [8, 128, 512], BF16)
    qs, kvk, kvv, kva = {}, {}, {}, {}
    for L in allL:
        inA, inBC = L in layersA, L in layersBC
        if inA and inBC:
            raise NotImplementedError("fused A+BC of the same layer needs the on-device gather")
        if inA:
            qs[L] = B.ext_out(f"qs{L}", [NQ, 128, tok], BF16)
            kvk[L] = B.ext_out(f"kvk{L}", [NKSEC, 128, tok], BF16)
            kvv[L] = B.ext_out(f"kvv{L}", [NVSEC, tok, 512], BF16)
            kva[L] = B.ext_out(f"kva{L}", [tok, 256], BF16)
            swk_o = B.ext_out(f"swk{L}", [2, 128, tok], BF16)
            swv_o = B.ext_out(f"swv{L}", [tok, 256], BF16)
            W[L]["swk_o"], W[L]["swv_o"] = swk_o, swv_o
        else:
            qs[L] = B.ext_in(f"qs{L}", [NQ, 128, tok], BF16)
            kvk[L] = B.ext_in(f"kvk{L}", [NKSEC, 128, NCORES * tok], BF16)
            kvv[L] = B.ext_in(f"kvv{L}", [NVSEC, NCORES * tok, 512], BF16)
            kva[L] = B.ext_in(f"kva{L}", [NCORES * tok, 256], BF16)
            W[L]["swk_i"] = B.ext_in(f"swk{L}", [2, 128, tok + 256], BF16)
            W[L]["swv_i"] = B.ext_in(f"swv{L}", [tok + 256, 256], BF16)
    if layersBC:
        xout = B.ext_out("xout", [D, tok])
        if os.environ.get("KDBG"):
            brs = B.ext_out("brs", [16, 128, tok], BF16)
        else:
            brs = nc.dram_tensor("brs", [16, 128, tok], BF16)
    if KSTOP <= 1:
        xout_dbg = B.ext_out("xdbg", [D, tok], BF16)

    B.setup_sbuf()
    oX = B.carve(32 * 1024)
    oY = B.carve(32 * 1024)
    oR1 = B.carve(64 * 1024)
    oW = B.carve(48 * 1024)
    oMF = B.carve(16 * 1024)
    oMB = B.carve(8 * 1024)
    oC = B.carve(5 * 1024)
    X = B.vf(oX, 16 * 512).rearrange("p (c t) -> p c t", c=16)
    Y = B.vf(oY, 16 * 512).rearrange("p (c t) -> p c t", c=16)
    H = B.vb(oR1, 16 * 512).rearrange("p (c t) -> p c t", c=16)
    ACTB = B.vb(oR1 + 16 * 1024, 48 * 512).rearrange("p (c t) -> p c t", c=48)
    KB = B.vb(oR1, NCORES * tok)
    VB = B.vb(oR1 + 32 * 1024, NCORES * tok).rearrange("p (b d) -> p b d", d=128)
    KRB = B.vb(oY, NCORES * tok)
    WS = [B.vb(oW + i * 16 * 1024, 8192) for i in range(3)]
    MF = [B.vf(oMF + i * 2048, 512) for i in range(6)]
    RS = [B.vf(oMF + (6 + i) * 2048, 512) for i in range(2)]
    MB = [B.vb(oMB + i * 1024, 512) for i in range(8)]
    gv = B.vf(oC, DEPTH * NGL)
    ones = B.vb(oC + 1536, 128)
    lamt = B.vf(oC + 1792, 8)
    sinkt = B.vf(oC + 1856, 16)
    lamw = B.vf(oC + 2048, 256)
    sm = B.vf(oC + 3072, 64)
    bufs = {}

    def mk_bufs():
        bufs.clear()
        for n in ("X", "Y", "H", "gv", "ones", "lamt", "sinkt", "lamw", "sm", "KRB", "CQN", "CKVN", "TB"):
            bufs[n] = P.buf(n)
        bufs["ACT"] = [P.buf(f"act{i}") for i in range(48)]
        bufs["Yc"] = [P.buf(f"y{i}") for i in range(16)]
        bufs["Xc"] = [P.buf(f"x{i}") for i in range(16)]
        bufs["Hc"] = [P.buf(f"h{i}") for i in range(16)]
        bufs["KP"] = [P.buf(f"kp{i}") for i in range(NCORES)]
        bufs["VP"] = [P.buf(f"vp{i}") for i in range(NCORES)]
        bufs["WS"] = [P.buf(f"ws{i}") for i in range(3)]
        bufs["MF"] = [P.buf(f"mf{i}") for i in range(6)]
        bufs["RS"] = [P.buf(f"rs{i}") for i in range(2)]
        bufs["MB"] = [P.buf(f"mb{i}") for i in range(8)]
        bufs["PS"] = [P.buf(f"ps{i}") for i in range(8)]
    mk_bufs()
    rr = {"ws": 0, "mf": 0, "mb": 0, "ps": 0, "rs": 0}

    def nxt(kind, n):
        i = rr[kind]
        rr[kind] = (i + 1) % n
        return i

    def mf():
        i = nxt("mf", 6)
        return MF[i], bufs["MF"][i]

    def rs():
        i = nxt("rs", 2)
        return RS[i], bufs["RS"][i]

    def mb():
        i = nxt("mb", 8)
        return MB[i], bufs["MB"][i]

    def psb(lo=0, hi=8):
        i = lo + nxt("ps", hi - lo) % (hi - lo)
        return B.ps[i], bufs["PS"][i]

    def barrier():
        P.barrier(lambda e: e.memset(sm[:, 60:64], 0.0))
        mk_bufs()

    def dma(eng, out, in_, reads, writes):
        P.op(eng, lambda e: e.dma_start(out=out, in_=in_), reads=reads, writes=writes, dma=True)

    def wload(src, ncols):
        i = nxt("ws", 3)
        dst = WS[i][:, 0:ncols]
        dma("pool", dst, src.ap_, [src.buf], [bufs["WS"][i]])
        return WS[i], bufs["WS"][i]

    def mm(ps, lhsT, rhs, start, stop, reads, pbuf):
        P.op("pe", lambda e: e.matmul(ps, lhsT, rhs, start=start, stop=stop), reads=reads, writes=[pbuf])

    def act(out, in_, func, reads, writes, scale=1.0):
        P.op("act", lambda e: e.activation(out=out, in_=in_, func=func, scale=scale), reads=reads, writes=writes)

    def tt(out, a, b, op, reads, writes):
        P.op("dve", lambda e: e.tensor_tensor(out=out, in0=a, in1=b, op=op), reads=reads, writes=writes)

    def stt(out, a, scalar, b, op0, op1, reads, writes):
        P.op("dve", lambda e: e.scalar_tensor_tensor(out=out, in0=a, scalar=scalar, in1=b, op0=op0, op1=op1),
             reads=reads, writes=writes)

    def ts(out, a, s1, s2, op0, op1, reads, writes):
        if op1 is None:
            P.op("dve", lambda e: e.tensor_scalar(out=out, in0=a, scalar1=s1, scalar2=None, op0=op0),
                 reads=reads, writes=writes)
        else:
            P.op("dve", lambda e: e.tensor_scalar(out=out, in0=a, scalar1=s1, scalar2=s2, op0=op0, op1=op1),
                 reads=reads, writes=writes)

    def rstd_from(srcs, dsum, plo=0, phi=8):
        pst, psbuf = psb(plo, phi)
        n = len(srcs)
        for i, (ap, bf_) in enumerate(srcs):
            sq, sqb = mb()
            act(sq, ap, AF.Square, [bf_], [sqb])
            mm(pst[:], ones[:, :], sq, i == 0, i == n - 1, [sqb, bufs["ones"]], psbuf)
        r, rb = rs()
        ts(r, pst[:], float(dsum * EPS), None, ALU.add, None, [psbuf], [rb])
        act(r, r, AF.Sqrt, [rb], [rb])
        P.op("dve", lambda e: e.reciprocal(out=r, in_=r), reads=[rb], writes=[rb])
        return r, rb

    def gcol(L, base, c=0):
        j = L * NGL + base + c
        return gv[:, j:j + 1]

    dma("sp", gv, gv_d.ap(), [], [bufs["gv"]])
    if KSTOP >= -1:
        P.op("dve", lambda e: e.memset(ones, 1.0), writes=[bufs["ones"]])
    for L in range(DEPTH if KSTOP >= 0 else 0):
        b0 = L * NGL
        for (lo, hi, n) in ((0, 80, 2048), (80, 88, 512), (88, 93, 128)):
            ts(gv[:, b0 + lo:b0 + hi], gv[:, b0 + lo:b0 + hi], float(math.sqrt(n)), None, ALU.mult, None,
               [bufs["gv"]], [bufs["gv"]])

    conv_order = []
    for L in layersBC:
        conv_order += [(L, n) for n in ("wg", "wb", "wo", "wfi", "wfo", "wpl", "wpg")]
    for L in layersA:
        conv_order += [(L, n) for n in ("winx", "wuq", "wukv")]
    if layersBC and layersA:
        pass
    for (L, n) in conv_order:
        ext = W[L][n]
        G, _, N = ext.shape
        twin = nc.dram_tensor(f"{n}{L}_bf", [G, 128, N], BF16)
        wbufs = [P.buf(f"wcv_{n}{L}_{g}", persistent=True) for g in range(G)]
        for g in range(G):
            dma("pool", twin.ap()[g], ext.ap()[g], [], [wbufs[g]])
        W[L][n] = WT(twin, wbufs)

    def load_x_tile(t, src):
        dma("sp", X, src.ap()[:, t * TT:(t + 1) * TT].rearrange("(c p) t -> p c t", p=128),
            [B.db((src.name, t))], bufs["Xc"] + [bufs["X"]])

    def norm_to_h(L, gbase):
        r, rb = rstd_from([(X[:, c, :], bufs["Xc"][c]) for c in range(16)], 2048)
        for c in range(16):
            stt(H[:, c, :], X[:, c, :], gcol(L, gbase, c), r, ALU.mult, ALU.mult,
                [bufs["Xc"][c], rb, bufs["gv"]], [bufs["Hc"][c]])

    def phase_A(L, xsrc):
        w = W[L]
        TBo = oY + 16 * 1024
        TB = [B.vf(TBo + i * 2048, 512) for i in range(6)]
        CQ = [Y[:, c, :] for c in range(4)]
        CKV = [Y[:, 4 + c, :] for c in range(4)]
        CQN = ACTB[:, 0:4, :]
        CKVN = ACTB[:, 4:8, :]
        for t in range(nt):
            tsl = slice(t * TT, (t + 1) * TT)
            load_x_tile(t, xsrc)
            norm_to_h(L, G_MIXPRE)
            dma("sp", B.vf(TBo, 6 * 512).rearrange("p (k t) -> p k t", k=6),
                tabs_d.ap()[:, :, tsl].rearrange("k p t -> p k t"), [], [bufs["TB"]])

            def proj_chunk(wv, wb_, ci, src3, srcbufs, nk, m=128):
                pst, pb = psb()
                for kc in range(nk):
                    mm(pst[0:m, :], wv[:, kc, ci * 128:ci * 128 + m], src3[:, kc, :], kc == 0, kc == nk - 1,
                       [wb_, srcbufs[kc]], pb)
                return pst, pb

            def rope_out(px, pxb, psw, pswb, ci_, si_, extra=None):
                t1, t1b = mf()
                t2, t2b = mf()
                o, ob = mb()
                if extra is None:
                    tt(t1, px[:], TB[ci_], ALU.mult, [pxb, bufs["TB"]], [t1b])
                    tt(t2, psw[:], TB[si_], ALU.mult, [pswb, bufs["TB"]], [t2b])
                    tt(o, t1, t2, ALU.add, [t1b, t2b], [ob])
                else:
                    g, gs, r, rb = extra
                    stt(t1, px[:], g, TB[ci_], ALU.mult, ALU.mult, [pxb, bufs["TB"], bufs["gv"]], [t1b])
                    stt(t2, psw[:], gs, TB[si_], ALU.mult, ALU.mult, [pswb, bufs["TB"], bufs["gv"]], [t2b])
                    tt(t1, t1, t2, ALU.add, [t1b, t2b], [t1b])
                    tt(o, t1, r, ALU.mult, [t1b, rb], [ob])
                return o, ob

            def store(dst_ap, o, ob, key):
                dma("sp", dst_ap, o, [ob], [B.db(key)])

            Hb = bufs["Hc"]
            if KSTOP <= 1:
                dma("sp", xout_dbg.ap()[:, tsl].rearrange("(c p) t -> p c t", p=128), H, Hb, [B.db(("dbg", t))])
                return
            for g in range(2):
                wv, wbf = wload(w["winx"].ap()[g], 8192)
                wv3 = wv.rearrange("p (k c) -> p k c", k=16)
                for ci in range(4):
                    pst, pb = proj_chunk(wv3, wbf, ci, H, Hb, 16)
                    act(Y[:, g * 4 + ci, :], pst[:], AF.Copy, [pb], [bufs["Yc"][g * 4 + ci]])
            for g, (gb, dstn, nb) in enumerate(((G_QA, CQN, "CQN"), (G_KVA, CKVN, "CKVN"))):
                r, rb = rstd_from([(Y[:, g * 4 + c, :], bufs["Yc"][g * 4 + c]) for c in range(4)], 512)
                for c in range(4):
                    stt(dstn[:, c, :], Y[:, g * 4 + c, :], gcol(L, gb, c), r, ALU.mult, ALU.mult,
                        [bufs["Yc"][g * 4 + c], rb, bufs["gv"]], [bufs["ACT"][g * 4 + c]])
            if KSTOP <= 2:
                return
            pair_list = [("kr", 0)] + [("dq", h) for h in range(4)] + [("dk", h) for h in range(4)] + \
                        [("sq", h) for h in range(4)] + [("sk", h) for h in range(2)] + \
                        [("aq", h) for h in range(4)] + [("ak", h) for h in range(2)]
            for g in range(2, 13):
                ncols = 8192
                wv, wbf = wload(w["winx"].ap()[g], ncols)
                gc = 512 if g < 12 else 512
                wv3 = wv.rearrange("p (k c) -> p k c", k=16)
                npairs = 2 if g < 12 else 1
                for pi in range(npairs):
                    kind, h = pair_list[(g - 2) * 2 + pi]
                    px, pxb = proj_chunk(wv3, wbf, pi * 2, H, Hb, 16)
                    psw, pswb = proj_chunk(wv3, wbf, pi * 2 + 1, H, Hb, 16)
                    if kind == "kr":
                        o, ob = rope_out(px, pxb, psw, pswb, 0, 1)
                        store(kvk[L].ap()[4][:, tsl], o, ob, ("kvk", L))
                    elif kind in ("dq", "dk"):
                        o, ob = rope_out(px, pxb, psw, pswb, 0, 1)
                        if kind == "dq":
                            store(qs[L].ap()[6 + h][:, tsl], o, ob, ("qs", L))
                        else:
                            store(kvk[L].ap()[5 + h][:, tsl], o, ob, ("kvk", L))
                    elif kind in ("sq", "sk"):
                        o, ob = rope_out(px, pxb, psw, pswb, 2, 3)
                        if kind == "sq":
                            store(qs[L].ap()[10 + h][:, tsl], o, ob, ("qs", L))
                        else:
                            store(w["swk_o"].ap()[h][:, tsl], o, ob, ("swk", L))
                    else:
                        xs, xsb = mf()
                        act(xs, px[:], AF.Copy, [pxb], [xsb])
                        r, rb = rstd_from([(xs, xsb)], 128)
                        gq = G_AXQ if kind == "aq" else G_AXK
                        o, ob = rope_out(px, pxb, psw, pswb, 4, 5, extra=(gcol(L, gq), gcol(L, gq + 1), r, rb))
                        if kind == "aq":
                            store(qs[L].ap()[14 + h][:, tsl], o, ob, ("qs", L))
                        else:
                            store(kvk[L].ap()[9 + h][:, tsl], o, ob, ("kvk", L))
            if KSTOP <= 3:
                return
            for g in (13, 14):
                wv, wbf = wload(w["winx"].ap()[g], 8192)
                wv3 = wv.rearrange("p (k c) -> p k c", k=16)
                for sub in range(4):
                    pst, pb = psb()
                    for kc in range(16):
                        mm(pst[:], H[:, kc, sub * 128:(sub + 1) * 128], wv3[:, kc, :], kc == 0, kc == 15,
                           [wbf, Hb[kc]], pb)
                    o, ob = mb()
                    act(o, pst[:], AF.Copy, [pb], [ob])
                    rows = slice(t * TT + sub * 128, t * TT + (sub + 1) * 128)
                    if g == 13:
                        store(kvv[L].ap()[1][rows, :], o, ob, ("kvv", L))
                    else:
                        store(w["swv_o"].ap()[rows, :], o[:, 0:256], ob, ("swv", L))
                        store(kva[L].ap()[rows, :], o[:, 256:512], ob, ("kva", L))
            if KSTOP <= 4:
                return
            CQb = bufs["ACT"][0:4]
            CKb = bufs["ACT"][4:8]
            wv, wbf = wload(w["wuq"].ap()[0], 2048)
            wv3 = wv[:, 0:2048].rearrange("p (k c) -> p k c", k=4)
            for h in range(4):
                pst, pb = proj_chunk(wv3, wbf, h, CQN, CQb, 4)
                o, ob = mb()
                act(o, pst[:], AF.Copy, [pb], [ob])
                store(qs[L].ap()[h][:, tsl], o, ob, ("qs", L))
            wv, wbf = wload(w["wuq"].ap()[1], 2048)
            wv3 = wv[:, 0:2048].rearrange("p (k c) -> p k c", k=4)
            for pi in range(2):
                px, pxb = proj_chunk(wv3, wbf, pi * 2, CQN, CQb, 4)
                psw, pswb = proj_chunk(wv3, wbf, pi * 2 + 1, CQN, CQb, 4)
                o, ob = rope_out(px, pxb, psw, pswb, 0, 1)
                store(qs[L].ap()[4 + pi][:, tsl], o, ob, ("qs", L))
            wv, wbf = wload(w["wukv"].ap()[0], 2048)
            wv3 = wv[:, 0:2048].rearrange("p (k c) -> p k c", k=4)
            for h in range(4):
                pst, pb = proj_chunk(wv3, wbf, h, CKVN, CKb, 4)
                o, ob = mb()
                act(o, pst[:], AF.Copy, [pb], [ob])
                store(kvk[L].ap()[h][:, tsl], o, ob, ("kvk", L))
            wv, wbf = wload(w["wukv"].ap()[1], 2048)
            wv3 = wv[:, 0:2048].rearrange("p (k c) -> p k c", k=4)
            for sub in range(4):
                pst, pb = psb()
                for kc in range(4):
                    mm(pst[:], CKVN[:, kc, sub * 128:(sub + 1) * 128], wv3[:, kc, :], kc == 0, kc == 3,
                       [wbf, CKb[kc]], pb)
                o, ob = mb()
                act(o, pst[:], AF.Copy, [pb], [ob])
                rows = slice(t * TT + sub * 128, t * TT + (sub + 1) * 128)
                store(kvv[L].ap()[0][rows, :], o, ob, ("kvv", L))

    def phase_B(L):
        w = W[L]
        lam_init = 0.8 - 0.6 * math.exp(-0.3 * L)
        QA = [B.vb(oW + i * 8192, 4096).rearrange("p (c t) -> p c t", c=2) for i in range(2)]
        QAb = [bufs["WS"][0], bufs["WS"][1]]
        SWK = B.vb(oW + 16 * 1024, 2 * (tok + 256)).rearrange("p (h t) -> p h t", h=2)
        SWV = B.vb(oW + 26 * 1024, (tok // 128 + 2) * 256).rearrange("p (b d) -> p b d", d=256)
        dma("sp", lamw, lam_d.ap()[:, L * 256:(L + 1) * 256], [], [bufs["lamw"]])
        dma("sp", sinkt[:, 0:4], sink_d.ap()[:, L * 4:(L + 1) * 4], [], [bufs["sinkt"]])
        for i in range(2):
            t1, t1b = mf()
            P.op("dve", lambda e, i=i, t1=t1: e.tensor_tensor(out=t1[:, 0:64], in0=lamw[:, i * 128:i * 128 + 64],
                                                               in1=lamw[:, i * 128 + 64:i * 128 + 128], op=ALU.mult),
                 reads=[bufs["lamw"]], writes=[t1b])
            P.op("dve", lambda e, i=i, t1=t1: e.reduce_sum(out=sm[:, i:i + 1], in_=t1[:, 0:64],
                                                            axis=mybir.AxisListType.X),
                 reads=[t1b], writes=[bufs["sm"]])
        act(sm[:, 2:4], sm[:, 0:2], AF.Exp, [bufs["sm"]], [bufs["sm"]])
        tt(lamt[:, 0:1], sm[:, 2:3], sm[:, 3:4], ALU.subtract, [bufs["sm"]], [bufs["lamt"]])
        ts(lamt[:, 0:1], lamt[:, 0:1], float(lam_init), None, ALU.add, None, [bufs["lamt"]], [bufs["lamt"]])
        ts(lamt[:, 1:2], lamt[:, 0:1], -1.0, None, ALU.mult, None, [bufs["lamt"]], [bufs["lamt"]])
        act(sinkt[:, 4:8], sinkt[:, 0:4], AF.Exp, [bufs["sinkt"]], [bufs["sinkt"]])

        def finalize(o_ps, o_b, l_ps, l_b, extra_l=None):
            rl, rlb = mf()
            if extra_l is not None:
                ts(rl, l_ps[:], extra_l, None, ALU.add, None, [l_b, bufs["sinkt"]], [rlb])
                P.op("dve", lambda e: e.reciprocal(out=rl, in_=rl), reads=[rlb], writes=[rlb])
            else:
                P.op("dve", lambda e: e.reciprocal(out=rl, in_=l_ps[:]), reads=[l_b], writes=[rlb])
            o, ob = mf()
            tt(o, o_ps[:], rl, ALU.mult, [o_b, rlb], [ob])
            return o, ob

        def store_branch(chunk, t, o, ob):
            dma("sp", brs.ap()[chunk][:, t * TT:(t + 1) * TT], o, [ob], [B.db(("brs", t))])

        swmT = B.vb(oY, 8 * 512).rearrange("p (k t) -> p k t", k=8)
        dma("sp", swmT, swm_d.ap().rearrange("k p t -> p k t"), [], [bufs["Y"]])
        dma("sp", SWK, w["swk_i"].ap().rearrange("h p t -> p h t"), [], [bufs["WS"][2]])
        dma("sp", SWV, w["swv_i"].ap().rearrange("(b p) d -> p b d", p=128), [], [bufs["WS"][2]])
        for h in range(4):
            kvh = h // 2
            qi = h % 2
            dma("sp", QA[qi][:, 0, 0:tok], qs[L].ap()[10 + h], [B.db(("qs", L))], [QAb[qi]])
            for t in range(nt):
                o_ps, o_b = B.ps[4], bufs["PS"][4]
                l_ps, l_b = B.ps[5], bufs["PS"][5]
                for mi, m in enumerate(range(-1, 5)):
                    kb = t * 4 + m + 1
                    s_ps, s_b = psb(0, 4)
                    mm(s_ps[:], SWK[:, kvh, kb * 128:(kb + 1) * 128], QA[qi][:, 0, t * TT:(t + 1) * TT],
                       True, True, [bufs["WS"][2], QAb[qi]], s_b)
                    p, pb = mb()
                    act(p, s_ps[:], AF.Exp, [s_b], [pb], scale=128 ** -0.5)
                    mk = mi
                    if t == 0 and m == -1:
                        mk = 6
                    if t == nt - 1 and m == 4:
                        mk = 7
                    tt(p, p, swmT[:, mk, :], ALU.mult, [pb, bufs["Y"]], [pb])
                    mm(o_ps[:], SWV[:, kb, kvh * 128:(kvh + 1) * 128], p, mi == 0, mi == 5,
                       [bufs["WS"][2], pb], o_b)
                    mm(l_ps[:], ones[:, :], p, mi == 0, mi == 5, [bufs["ones"], pb], l_b)
                o, ob = finalize(o_ps, o_b, l_ps, l_b, extra_l=sinkt[:, 4 + h:5 + h])
                ob16, obb = mb()
                P.op("dve", lambda e, ob16=ob16, o=o: e.tensor_copy(out=ob16, in_=o), reads=[ob], writes=[obb])
                store_branch(8 + h, t, ob16, obb)

        def kv_src(kind, h):
            if kind == "mla":
                return kvk[L].ap()[h], kvv[L].ap()[0], slice(h * 128, (h + 1) * 128)
            if kind == "diff":
                return kvk[L].ap()[5 + h], kvv[L].ap()[1], slice(h * 128, (h + 1) * 128)
            kvh = h // 2
            return kvk[L].ap()[9 + kvh], kva[L].ap(), slice(kvh * 128, (kvh + 1) * 128)

        def load_piece(kind, h, r):
            ksrc, vsrc, vcols = kv_src(kind, h)
            dma("sp", KB[:, r * tok:(r + 1) * tok], ksrc[:, r * tok:(r + 1) * tok], [], [bufs["KP"][r]])
            dma("sp", VB[:, r * NKB:(r + 1) * NKB, :],
                vsrc[r * tok:(r + 1) * tok, vcols].rearrange("(b p) d -> p b d", p=128), [], [bufs["VP"][r]])

        def load_q(kind, h):
            qi = h % 2
            if kind == "mla":
                dma("sp", QA[qi][:, 0, 0:tok], qs[L].ap()[h], [B.db(("qs", L))], [QAb[qi]])
                dma("sp", QA[qi][:, 1, 0:tok], qs[L].ap()[4 + h // 2], [B.db(("qs", L))], [QAb[qi]])
            elif kind == "diff":
                dma("sp", QA[qi][:, 0, 0:tok], qs[L].ap()[6 + h], [B.db(("qs", L))], [QAb[qi]])
            else:
                dma("sp", QA[qi][:, 0, 0:tok], qs[L].ap()[14 + h], [B.db(("qs", L))], [QAb[qi]])

        def prefetch(kind, h):
            load_q(kind, h)
            for r in range(NCORES):
                load_piece(kind, h, r)

        def dense(kind, h, nxt_head):
            qi = h % 2
            scale = {"mla": 192 ** -0.5, "diff": 64 ** -0.5, "ax": 128 ** -0.5}[kind]
            nsm = 2 if kind == "diff" else 1
            pl = (h % 2) * 64
            for t in range(nt):
                last_t = (t == nt - 1 and nxt_head is not None)
                if last_t:
                    load_q(*nxt_head)
                qsl = slice(t * TT, (t + 1) * TT)
                acc = [(B.ps[4], bufs["PS"][4], B.ps[5], bufs["PS"][5]), (B.ps[6], bufs["PS"][6], B.ps[7], bufs["PS"][7])]
                nblk = NCORES * NKB
                pend = None

                def scores(bi):
                    r = bi // NKB
                    ksl = slice(bi * 128, (bi + 1) * 128)
                    outs = []
                    for s in range(nsm):
                        s_ps, s_b = psb(0, 4)
                        if kind == "mla":
                            mm(s_ps[:], KB[:, ksl], QA[qi][:, 0, qsl], True, False, [bufs["KP"][r], QAb[qi]], s_b)
                            mm(s_ps[:], KRB[pl:pl + 64, ksl], QA[qi][pl:pl + 64, 1, qsl], False, True,
                               [bufs["KRB"], QAb[qi]], s_b)
                        elif kind == "diff":
                            mm(s_ps[:], KB[s * 64:s * 64 + 64, ksl], QA[qi][s * 64:s * 64 + 64, 0, qsl], True, True,
                               [bufs["KP"][r], QAb[qi]], s_b)
                        else:
                            mm(s_ps[:], KB[:, ksl], QA[qi][:, 0, qsl], True, True, [bufs["KP"][r], QAb[qi]], s_b)
                        outs.append((s_ps, s_b))
                    return outs
                cur = scores(0)
                for bi in range(nblk):
                    nx = scores(bi + 1) if bi + 1 < nblk else None
                    r = bi // NKB
                    for s in range(nsm):
                        s_ps, s_b = cur[s]
                        p, pb = mb()
                        act(p, s_ps[:], AF.Exp, [s_b], [pb], scale=scale)
                        o_ps, o_b, l_ps, l_b = acc[s]
                        mm(o_ps[:], VB[:, bi, :], p, bi == 0, bi == nblk - 1, [bufs["VP"][r], pb], o_b)
                        mm(l_ps[:], ones[:, :], p, bi == 0, bi == nblk - 1, [bufs["ones"], pb], l_b)
                    if last_t and (bi + 1) % NKB == 0:
                        load_piece(nxt_head[0], nxt_head[1], r)
                    cur = nx
                if kind == "diff":
                    o1, o1b = finalize(*acc[0])
                    o2, o2b = finalize(*acc[1])
                    stt(o1, o2, lamt[:, 1:2], o1, ALU.mult, ALU.add, [o2b, o1b, bufs["lamt"]], [o1b])
                    r_, rb_ = rstd_from([(o1, o1b)], 128, 0, 4)
                    ob16, obb = mb()
                    stt(ob16, o1, gcol(L, G_SUBLN), r_, ALU.mult, ALU.mult, [o1b, rb_, bufs["gv"]], [obb])
                    ts(ob16, ob16, float(1.0 - lam_init), None, ALU.mult, None, [obb], [obb])
                    store_branch(4 + h, t, ob16, obb)
                else:
                    o, ob = finalize(*acc[0])
                    ob16, obb = mb()
                    P.op("dve", lambda e, ob16=ob16, o=o: e.tensor_copy(out=ob16, in_=o), reads=[ob], writes=[obb])
                    store_branch((0 if kind == "mla" else 12) + h, t, ob16, obb)

        barrier()
        dma("sp", KRB, kvk[L].ap()[4], [], [bufs["KRB"]])
        heads = [(k_, h) for k_ in ("mla", "diff", "ax") for h in range(4)]
        prefetch(*heads[0])
        for i, (k_, h) in enumerate(heads):
            dense(k_, h, heads[i + 1] if i + 1 < len(heads) else None)

    def phase_C(L, xsrc, xdst):
        w = W[L]
        BR = ACTB[:, 0:16, :]
        MG = ACTB[:, 16:32, :]
        XB = ACTB[:, 32:48, :]
        PF = ACTB[:, 0:2, :]
        for t in range(nt):
            tsl = slice(t * TT, (t + 1) * TT)
            load_x_tile(t, xsrc)
            norm_to_h(L, G_MIXPRE)
            dma("sp", BR, brs.ap()[:, :, tsl].rearrange("c p t -> p c t"), [B.db(("brs", t))], bufs["ACT"][0:16])
            Hb = bufs["Hc"]
            for oc in range(16):
                wgv, wgb = wload(w["wg"].ap()[oc], 8192)
                wg4 = wgv.rearrange("p (j k c) -> p j k c", j=4, k=16)
                wbv, wbb = wload(w["wb"].ap()[oc], 2048)
                wb4 = wbv[:, 0:2048].rearrange("p (j k c) -> p j k c", j=4, k=4)
                acc, accb = mf()
                for j in range(4):
                    g_ps, g_b = psb()
                    for kc in range(16):
                        mm(g_ps[:], wg4[:, j, kc, :], H[:, kc, :], kc == 0, kc == 15, [wgb, Hb[kc]], g_b)
                    b_ps, b_b = psb()
                    for kc in range(4):
                        mm(b_ps[:], wb4[:, j, kc, :], BR[:, j * 4 + kc, :], kc == 0, kc == 3,
                           [wbb, bufs["ACT"][j * 4 + kc]], b_b)
                    sg, sgb = mf()
                    act(sg, g_ps[:], AF.Sigmoid, [g_b], [sgb])
                    if j == 0:
                        tt(acc, sg, b_ps[:], ALU.mult, [sgb, b_b], [accb])
                    else:
                        tt(sg, sg, b_ps[:], ALU.mult, [sgb, b_b], [sgb])
                        if j < 3:
                            tt(acc, acc, sg, ALU.add, [accb, sgb], [accb])
                        else:
                            tt(MG[:, oc, :], acc, sg, ALU.add, [accb, sgb], [bufs["ACT"][16 + oc]])

            def proj_to_Y(wkey, src3, srcb, nk, groups):
                for g in range(groups):
                    wv, wbf = wload(w[wkey].ap()[g], nk * 512)
                    wv3 = wv[:, 0:nk * 512].rearrange("p (k c) -> p k c", k=nk)
                    for ci in range(4):
                        pst, pb = psb()
                        for kc in range(nk):
                            mm(pst[:], wv3[:, kc, ci * 128:(ci + 1) * 128], src3[:, kc, :], kc == 0, kc == nk - 1,
                               [wbf, srcb[kc]], pb)
                        act(Y[:, g * 4 + ci, :], pst[:], AF.Copy, [pb], [bufs["Yc"][g * 4 + ci]])

            def resid_add(gbase):
                r, rb = rstd_from([(Y[:, c, :], bufs["Yc"][c]) for c in range(16)], 2048)
                for c in range(16):
                    t1, t1b = mf()
                    stt(t1, Y[:, c, :], gcol(L, gbase, c), r, ALU.mult, ALU.mult,
                        [bufs["Yc"][c], rb, bufs["gv"]], [t1b])
                    tt(X[:, c, :], X[:, c, :], t1, ALU.add, [bufs["Xc"][c], t1b], [bufs["Xc"][c]])

            def fin():
                dma("sp", xdst.ap()[:, tsl].rearrange("(c p) t -> p c t", p=128), X, bufs["Xc"], [B.db((xdst.name, t))])
            KC_ = int(os.environ.get("KSTOPC", "9"))
            proj_to_Y("wo", MG, bufs["ACT"][16:32], 16, 4)
            if KC_ == 0:
                for c in range(16):
                    P.op("dve", lambda e, c=c: e.tensor_copy(out=X[:, c, :], in_=Y[:, c, :]), reads=[bufs["Yc"][c]], writes=[bufs["Xc"][c]])
                fin()
                continue
            resid_add(G_MIXPOST)
            if KC_ == 1:
                fin()
                continue
            norm_to_h(L, G_FFNPRE)
            for g in range(22):
                wv, wbf = wload(w["wfi"].ap()[g], 8192)
                wv4 = wv.rearrange("p (s k c) -> p s k c", s=2, k=16)
                for ci in range(2):
                    hc = g * 2 + ci
                    g_ps, g_b = psb()
                    for kc in range(16):
                        mm(g_ps[:], wv4[:, 0, kc, ci * 128:(ci + 1) * 128], H[:, kc, :], kc == 0, kc == 15,
                           [wbf, Hb[kc]], g_b)
                    u_ps, u_b = psb()
                    for kc in range(16):
                        mm(u_ps[:], wv4[:, 1, kc, ci * 128:(ci + 1) * 128], H[:, kc, :], kc == 0, kc == 15,
                           [wbf, Hb[kc]], u_b)
                    sg, sgb = mf()
                    act(sg, g_ps[:], AF.Silu, [g_b], [sgb])
                    tt(ACTB[:, hc, :], sg, u_ps[:], ALU.mult, [sgb, u_b], [bufs["ACT"][hc]])
            for oc in range(16):
                wv, wbf = wload(w["wfo"].ap()[oc], 44 * 128)
                wv3 = wv[:, 0:44 * 128].rearrange("p (k c) -> p k c", k=44)
                pst, pb = psb()
                for kc in range(44):
                    mm(pst[:], wv3[:, kc, :], ACTB[:, kc, :], kc == 0, kc == 43, [wbf, bufs["ACT"][kc]], pb)
                act(Y[:, oc, :], pst[:], AF.Copy, [pb], [bufs["Yc"][oc]])
            resid_add(G_FFNPOST)
            if KC_ == 2:
                fin()
                continue
            for c in range(16):
                P.op("dve", lambda e, c=c: e.tensor_copy(out=XB[:, c, :], in_=X[:, c, :]),
                     reads=[bufs["Xc"][c]], writes=[bufs["ACT"][32 + c]])
            pf32, pfb = [], []
            for k in range(2):
                tf, tfb = mf()
                dma("sp", tf, w["pfm"].ap()[k * 128:(k + 1) * 128, tsl], [], [tfb])
                P.op("dve", lambda e, k=k, tf=tf: e.tensor_copy(out=PF[:, k, :], in_=tf), reads=[tfb],
                     writes=[bufs["ACT"][k]])
            WPL = ACTB[:, 8:16, :]
            wsrc = w["wpl"].ap()[0].rearrange("p (k t) -> p k t", k=8)
            dma("pool", WPL, wsrc.ap_, [wsrc.buf], bufs["ACT"][8:16])
            wp3 = WPL.rearrange("p (k a) t -> p k (a t)", k=2)
            wpb = bufs["ACT"][8]
            for g in range(4):
                wv, wbf = wload(w["wpg"].ap()[g], 8192)
                wv3 = wv.rearrange("p (k c) -> p k c", k=16)
                for ci in range(4):
                    oc = g * 4 + ci
                    g_ps, g_b = psb()
                    for kc in range(16):
                        mm(g_ps[:], wv3[:, kc, ci * 128:(ci + 1) * 128], XB[:, kc, :], kc == 0, kc == 15,
                           [wbf, bufs["ACT"][32 + kc]], g_b)
                    e_ps, e_b = psb()
                    for kc in range(2):
                        mm(e_ps[:], wp3[:, kc, oc * 128:(oc + 1) * 128], PF[:, kc, :], kc == 0, kc == 1,
                           bufs["ACT"][8:16] + [bufs["ACT"][kc]], e_b)
                    sg, sgb = mf()
                    act(sg, g_ps[:], AF.Sigmoid, [g_b], [sgb])
                    tt(Y[:, oc, :], sg, e_ps[:], ALU.mult, [sgb, e_b], [bufs["Yc"][oc]])
            resid_add(G_PLEPOST)
            dma("sp", xdst.ap()[:, tsl].rearrange("(c p) t -> p c t", p=128), X, bufs["Xc"], [B.db((xdst.name, t))])

    for (kind, L) in phases:
        if KSTOP <= 0:
            break
        if kind == "A":
            phase_A(L, xout if (layersBC and layersBC[0] < L) else xin)
            barrier()
        else:
            phase_B(L)
            barrier()
            phase_C(L, xin, xout)
            barrier()
    P.emit(nc, B.st)
    return nc


def _run(nc, in_maps):
    return run_bass_kernel_spmd(nc, in_maps, core_ids=list(range(NCORES))).results


_DBG = []


def forward(inp, tok, depth):
    S = tok * NCORES
    bf = ml_dtypes.bfloat16
    x = np.ascontiguousarray(inp["x"][0].T)
    gv = _gains(inp)
    lamv = np.ascontiguousarray(np.broadcast_to(inp["diff_lambda"].reshape(1, -1), (128, DEPTH * 256))).astype(np.float32)
    sinkv = np.ascontiguousarray(np.broadcast_to(inp["swa_sink"].reshape(1, -1), (128, DEPTH * 4))).astype(np.float32)
    tabs = _rope_tabs(S)
    xs = [np.ascontiguousarray(x[:, c * tok:(c + 1) * tok]) for c in range(NCORES)]
    hw = _host_weights(inp, list(range(depth)))
    prev = None
    for k in range(depth + 1):
        phases = []
        if k > 0:
            phases.append(("BC", k - 1))
        if k < depth:
            phases.append(("A", k))
        nc = build_program(tok, phases, final=(k == depth), first=(k == 0))
        maps = []
        for c in range(NCORES):
            m = {"xin": xs[c], "gv": gv}
            if k > 0:
                m["lamv"] = lamv
                m["sinkv"] = sinkv
            if k < depth:
                m["tabs"] = np.ascontiguousarray(tabs[:, :, c * tok:(c + 1) * tok])
                for nm in ("winx", "wuq", "wukv"):
                    m[f"{nm}{k}"] = hw[k][nm]
            if k > 0:
                L = k - 1
                for nm in ("wg", "wb", "wo", "wfi", "wfo", "wpg", "wpl"):
                    m[f"{nm}{L}"] = hw[L][nm]
                m[f"pfm{L}"] = np.ascontiguousarray(inp["p"][L, 0, c * tok:(c + 1) * tok, :].T)
                m["swm"] = _swa_masks(c, NCORES, tok // TT)
                m[f"qs{L}"] = prev[c][f"qs{L}"]
                m[f"kvk{L}"] = gat["kvk"]
                m[f"kvv{L}"] = gat["kvv"]
                m[f"kva{L}"] = gat["kva"]
                m[f"swk{L}"] = np.ascontiguousarray(gat["swk_pad"][:, :, c * tok:c * tok + tok + 256])
                m[f"swv{L}"] = np.ascontiguousarray(gat["swv_pad"][c * tok:c * tok + tok + 256])
            maps.append(m)
        res = _run(nc, maps)
        if os.environ.get("KDBG"):
            _DBG.append(res)
        if k > 0:
            xs = [res[c]["xout"] for c in range(NCORES)]
        if k < depth:
            gat = {
                "kvk": np.ascontiguousarray(np.concatenate([res[c][f"kvk{k}"] for c in range(NCORES)], 2)),
                "kvv": np.ascontiguousarray(np.concatenate([res[c][f"kvv{k}"] for c in range(NCORES)], 1)),
                "kva": np.ascontiguousarray(np.concatenate([res[c][f"kva{k}"] for c in range(NCORES)], 0)),
            }
            swk = np.concatenate([res[c][f"swk{k}"] for c in range(NCORES)], 2)
            swv = np.concatenate([res[c][f"swv{k}"] for c in range(NCORES)], 0)
            gat["swk_pad"] = np.pad(swk, ((0, 0), (0, 0), (128, 128)))
            gat["swv_pad"] = np.pad(swv, ((128, 128), (0, 0)))
        prev = res
    out = np.concatenate(xs, 1).T
    return np.ascontiguousarray(out)[None].astype(np.float32)


def kernel(**inputs):
    inp = {k: np.asarray(v) for k, v in inputs.items()}
    return forward(inp, 2048, DEPTH)
```
